# Optimizing a Trainium2 kernel written in Bass

```python
import math
import jax, jax.numpy as jnp
from jax import lax
import numpy as np

D_MODEL = 1024
BATCH = 8
SEQ = 4096
DEPTH = 2

D_RNN = D_MODEL
N_RNN_BLOCKS = 8
RNN_BLOCK = D_RNN // N_RNN_BLOCKS
RNN_CONV = 4
LRU_C = 8.0
N_HEADS = 8
HEAD_DIM = 128
D_ATTN = N_HEADS * HEAD_DIM
MOBA_BLOCK = 256
MOBA_TOPK = 3
Q_CHUNK = 16
NUM_BUCKETS = 32
MAX_DISTANCE = 1024
D_FF = 3 * D_MODEL
FFN_CONV = 3
EPS = 1e-6
NEG = -1e30
D_IN = 2 * D_RNN + 3 * D_ATTN + 2 * D_MODEL

kernel_name = "hybrid_rglru_moba_convffn"


def rmsnorm(x, g):
    xf = x.astype(jnp.float32)
    y = xf * lax.rsqrt(jnp.mean(xf * xf, axis=-1, keepdims=True) + EPS)
    return (y * g.astype(jnp.float32)).astype(x.dtype)


def causal_dwconv(x, w, b):
    width = w.shape[0]
    y = lax.conv_general_dilated(
        x, w[:, None, :].astype(x.dtype), window_strides=(1,), padding=[(width - 1, 0)],
        dimension_numbers=("NWC", "WIO", "NWC"), feature_group_count=x.shape[-1])
    return y + b.astype(x.dtype)


def block_diag(x, w, b):
    bsz, s, _ = x.shape
    xb = x.reshape(bsz, s, N_RNN_BLOCKS, RNN_BLOCK)
    y = jnp.einsum("bsnd,nde->bsne", xb, w) + b
    return y.reshape(bsz, s, D_RNN)


def rg_lru(x, wa, ba, wx, bx, lam):
    r = jax.nn.sigmoid(block_diag(x, wa, ba)).astype(jnp.float32)
    i = jax.nn.sigmoid(block_diag(x, wx, bx)).astype(jnp.float32)
    log_a = LRU_C * r * jax.nn.log_sigmoid(lam.astype(jnp.float32))
    a = jnp.exp(log_a)
    u = jnp.sqrt(-jnp.expm1(2.0 * log_a)) * i * x.astype(jnp.float32)

    def step(h, au):
        a_t, u_t = au
        h = a_t * h + u_t
        return h, h

    h0 = jnp.zeros((x.shape[0], x.shape[2]), jnp.float32)
    _, hs = lax.scan(step, h0, (jnp.swapaxes(a, 0, 1), jnp.swapaxes(u, 0, 1)))
    return jnp.swapaxes(hs, 0, 1).astype(x.dtype)


def t5_bucket(rel):
    n = jnp.maximum(rel, 0)
    max_exact = NUM_BUCKETS // 2
    nf = jnp.maximum(n, 1).astype(jnp.float32)
    large = max_exact + (jnp.log(nf / max_exact) / math.log(MAX_DISTANCE / max_exact)
                         * (NUM_BUCKETS - max_exact)).astype(jnp.int32)
    large = jnp.minimum(large, NUM_BUCKETS - 1)
    return jnp.where(n < max_exact, n, large)


def moba_attention(q, k, v, rel_bias):
    bsz, s = q.shape[0], q.shape[1]
    nb = -(-s // MOBA_BLOCK)
    pad = nb * MOBA_BLOCK - s
    q = q.transpose(0, 2, 1, 3)
    k = k.transpose(0, 2, 1, 3)
    v = v.transpose(0, 2, 1, 3)
    kb = jnp.pad(k, ((0, 0), (0, 0), (0, pad), (0, 0))).reshape(bsz, N_HEADS, nb, MOBA_BLOCK, HEAD_DIM)
    vb = jnp.pad(v, ((0, 0), (0, 0), (0, pad), (0, 0))).reshape(bsz, N_HEADS, nb, MOBA_BLOCK, HEAD_DIM)

    k_mean = jnp.mean(kb.astype(jnp.float32), axis=3)
    pos = jnp.arange(s, dtype=jnp.int32)
    q_blk = pos // MOBA_BLOCK
    gate = jnp.einsum("bhsd,bhnd->bhsn", q.astype(jnp.float32), k_mean)
    past = jnp.arange(nb, dtype=jnp.int32)[None, :] < q_blk[:, None]
    gate = jnp.where(past[None, None], gate, NEG)
    k_sel = min(MOBA_TOPK, nb)
    _, idx = lax.top_k(gate, k_sel)
    valid = jnp.arange(k_sel, dtype=jnp.int32)[None, :] < q_blk[:, None]

    nc = s // Q_CHUNK
    q_ch = q.reshape(bsz, N_HEADS, nc, Q_CHUNK, HEAD_DIM).transpose(2, 0, 1, 3, 4)
    i_ch = idx.reshape(bsz, N_HEADS, nc, Q_CHUNK, k_sel).transpose(2, 0, 1, 3, 4)
    v_ch = valid.reshape(nc, Q_CHUNK, k_sel)
    scale = HEAD_DIM ** -0.5
    tab = rel_bias.astype(jnp.float32).T
    gather_blocks = jax.vmap(jax.vmap(lambda blocks, i: blocks[i]))
    lookup = jax.vmap(lambda t, bk: t[bk], in_axes=(0, 1), out_axes=1)
    arange_blk = jnp.arange(MOBA_BLOCK, dtype=jnp.int32)

    def chunk(args):
        qc, ic, vc, c = args
        qpos = c * Q_CHUNK + jnp.arange(Q_CHUNK, dtype=jnp.int32)
        blk = (c * Q_CHUNK) // MOBA_BLOCK
        k_own = lax.dynamic_index_in_dim(kb, blk, axis=2, keepdims=False)
        v_own = lax.dynamic_index_in_dim(vb, blk, axis=2, keepdims=False)
        rel_own = qpos[:, None] - (blk * MOBA_BLOCK + arange_blk)[None, :]
        l_own = (jnp.einsum("bhqd,bhtd->bhqt", qc, k_own).astype(jnp.float32) * scale
                 + tab[:, t5_bucket(rel_own)][None])
        l_own = jnp.where((rel_own >= 0)[None, None], l_own, NEG)
        ks = gather_blocks(kb, ic)
        vs = gather_blocks(vb, ic)
        rel_sel = qpos[None, None, :, None, None] - (ic[..., None] * MOBA_BLOCK + arange_blk)
        l_sel = (jnp.einsum("bhqd,bhqktd->bhqkt", qc, ks).astype(jnp.float32) * scale
                 + lookup(tab, t5_bucket(rel_sel)))
        l_sel = jnp.where(vc[None, None, :, :, None], l_sel, NEG)
        logits = jnp.concatenate(
            [l_sel.reshape(bsz, N_HEADS, Q_CHUNK, k_sel * MOBA_BLOCK), l_own], axis=-1)
        p = jax.nn.softmax(logits, axis=-1)
        p_sel = p[..., : k_sel * MOBA_BLOCK].reshape(bsz, N_HEADS, Q_CHUNK, k_sel, MOBA_BLOCK).astype(vs.dtype)
        p_own = p[..., k_sel * MOBA_BLOCK:].astype(v_own.dtype)
        return (jnp.einsum("bhqkt,bhqktd->bhqd", p_sel, vs)
                + jnp.einsum("bhqt,bhtd->bhqd", p_own, v_own))

    out = lax.map(chunk, (q_ch, i_ch, v_ch, jnp.arange(nc, dtype=jnp.int32)))
    return out.transpose(1, 0, 3, 2, 4).reshape(bsz, s, D_ATTN)


def setup_inputs(seed: int = 0) -> dict:
    key = jax.random.key(seed)
    ks = jax.random.split(key, 24)
    f32 = jnp.float32
    nrm = lambda k, shape, fan: jax.random.normal(k, shape, f32) * fan ** -0.5
    u = jax.random.uniform(ks[10], (DEPTH, D_RNN), f32, 0.9, 0.999)
    base = u ** (1.0 / LRU_C)
    return {
        "x": jax.random.normal(ks[0], (BATCH, SEQ, D_MODEL), f32),
        "norm1_g": 1.0 + 0.05 * jax.random.normal(ks[1], (DEPTH, D_MODEL), f32),
        "w_in": nrm(ks[2], (DEPTH, D_MODEL, D_IN), D_MODEL),
        "rnn_conv_w": nrm(ks[3], (DEPTH, RNN_CONV, D_RNN), RNN_CONV),
        "rnn_conv_b": 0.02 * jax.random.normal(ks[4], (DEPTH, D_RNN), f32),
        "lru_wa": nrm(ks[5], (DEPTH, N_RNN_BLOCKS, RNN_BLOCK, RNN_BLOCK), RNN_BLOCK),
        "lru_ba": 0.02 * jax.random.normal(ks[6], (DEPTH, N_RNN_BLOCKS, RNN_BLOCK), f32),
        "lru_wx": nrm(ks[7], (DEPTH, N_RNN_BLOCKS, RNN_BLOCK, RNN_BLOCK), RNN_BLOCK),
        "lru_bx": 0.02 * jax.random.normal(ks[8], (DEPTH, N_RNN_BLOCKS, RNN_BLOCK), f32),
        "lru_lambda": jnp.log(base) - jnp.log1p(-base),
        "w_rnn_out": nrm(ks[11], (DEPTH, D_RNN, D_MODEL), D_RNN),
        "w_attn_out": nrm(ks[12], (DEPTH, D_ATTN, D_MODEL), D_ATTN),
        "rel_bias": 0.5 * jax.random.normal(ks[13], (NUM_BUCKETS, N_HEADS), f32),
        "w_o": nrm(ks[14], (DEPTH, D_MODEL, D_MODEL), D_MODEL),
        "norm2_g": 1.0 + 0.05 * jax.random.normal(ks[15], (DEPTH, D_MODEL), f32),
        "w_up": nrm(ks[16], (DEPTH, D_MODEL, 2 * D_FF), D_MODEL),
        "ffn_conv_w": nrm(ks[17], (DEPTH, FFN_CONV, D_FF), FFN_CONV),
        "ffn_conv_b": 0.02 * jax.random.normal(ks[18], (DEPTH, D_FF), f32),
        "w_down": nrm(ks[19], (DEPTH, D_FF, D_MODEL), D_FF),
        "final_g": 1.0 + 0.05 * jax.random.normal(ks[20], (D_MODEL,), f32),
    }


def reference(x, norm1_g, w_in, rnn_conv_w, rnn_conv_b, lru_wa, lru_ba, lru_wx, lru_bx,
              lru_lambda, w_rnn_out, w_attn_out, rel_bias, w_o, norm2_g, w_up,
              ffn_conv_w, ffn_conv_b, w_down, final_g):
    bsz, s, _ = x.shape
    splits = np.cumsum([D_RNN, D_RNN, D_ATTN, D_ATTN, D_ATTN, D_MODEL]).tolist()
    for l in range(DEPTH):
        h = rmsnorm(x, norm1_g[l])
        proj = h @ w_in[l]
        xr, yr, q, k, v, ga, gb = jnp.split(proj, splits, axis=-1)
        xr = causal_dwconv(xr, rnn_conv_w[l], rnn_conv_b[l])
        hr = rg_lru(xr, lru_wa[l], lru_ba[l], lru_wx[l], lru_bx[l], lru_lambda[l])
        ya = (hr * jax.nn.gelu(yr)) @ w_rnn_out[l]
        att = moba_attention(q.reshape(bsz, s, N_HEADS, HEAD_DIM),
                             k.reshape(bsz, s, N_HEADS, HEAD_DIM),
                             v.reshape(bsz, s, N_HEADS, HEAD_DIM), rel_bias)
        yb = att @ w_attn_out[l]
        mixed = jax.nn.sigmoid(ga) * ya + jax.nn.sigmoid(gb) * yb
        x = x + mixed @ w_o[l]
        h2 = rmsnorm(x, norm2_g[l])
        g, val = jnp.split(h2 @ w_up[l], 2, axis=-1)
        g = causal_dwconv(g, ffn_conv_w[l], ffn_conv_b[l])
        x = x + (jax.nn.gelu(g) * val) @ w_down[l]
    return rmsnorm(x, final_g)
```

```python
import contextlib
import math
import os
import numpy as np
import concourse.bass as bass
import concourse.mybir as mybir
from concourse.bass_utils import run_bass_kernel_spmd

F32 = mybir.dt.float32
BF16 = mybir.dt.bfloat16
AF = mybir.ActivationFunctionType
ALU = mybir.AluOpType
AX = mybir.AxisListType

D = 1024
S = 4096
NL = 2
NH = 8
NB = 16
BLK = 256
DFF = 3072
EPS = 1e-6
NEGM = -30000.0
GE = 1408
FR = 1536
PCL = 176


class Buf:
    __slots__ = ("w", "r", "name")

    def __init__(self, name=""):
        self.w = {}
        self.r = {}
        self.name = name


class Slot:
    def __init__(self, key, sem):
        self.key = key
        self.sem = sem
        self.cnt = 0


class Sched:
    def __init__(self, nc, es):
        self.nc = nc
        self.es = es
        self.E = {}
        for nm, h in [("pe", nc.tensor), ("act", nc.scalar), ("dve", nc.vector), ("pool", nc.gpsimd), ("sp", nc.sync)]:
            sem = es.enter_context(nc.semaphore("sem_" + nm))
            self.E[nm] = dict(h=h, sem=sem, cnt=0, seen={})
        self.slots = {}
        self.nwait = 0
        self.nins = 0

    def slot(self, key):
        if key not in self.slots:
            self.slots[key] = Slot(key, self.es.enter_context(self.nc.semaphore("sl_" + key)))
        return self.slots[key]

    def _deps(self, reads, writes, acc):
        d = {}

        def add(dd):
            for k, sv in dd.items():
                if k not in d or d[k][1] < sv[1]:
                    d[k] = sv
        for b in reads:
            add(b.w)
        for b in writes:
            add(b.w)
            add(b.r)
        for b in acc:
            add(b.r)
        return d

    def _wait(self, eng, d):
        st = self.E[eng]
        seen = st["seen"]
        for key, (sem, val) in d.items():
            if key == "pe" and eng == "pe":
                continue
            if seen.get(key, 0) >= val:
                continue
            st["h"].wait_ge(sem, val)
            seen[key] = val
            self.nwait += 1

    def _mark(self, key, tok, reads, writes, acc):
        for b in writes:
            b.w = {key: tok}
            b.r = {}
        for b in acc:
            if key not in b.w or b.w[key][1] < tok[1]:
                b.w[key] = tok
        for b in reads:
            if key not in b.r or b.r[key][1] < tok[1]:
                b.r[key] = tok

    def op(self, eng, fns, reads=(), writes=(), acc=()):
        self._wait(eng, self._deps(reads, writes, acc))
        st = self.E[eng]
        if callable(fns):
            fns = [fns]
        ins = None
        for f in fns:
            ins = f(st["h"])
            self.nins += 1
        st["cnt"] += 1
        ins.then_inc(st["sem"], 1)
        self._mark(eng, (st["sem"], st["cnt"]), reads, writes, acc)

    def dma(self, q, slot, out, in_, reads=(), writes=(), acc=()):
        self._wait(q, self._deps(reads, writes, acc))
        self.E[q]["h"].dma_start(out=out, in_=in_).then_inc(slot.sem, 16)
        self.nins += 1
        slot.cnt += 16
        self._mark(slot.key, (slot.sem, slot.cnt), reads, writes, acc)

    def barrier(self, engines=None):
        allt = {k: (st["sem"], st["cnt"]) for k, st in self.E.items() if st["cnt"] > 0}
        for sl in self.slots.values():
            if sl.cnt:
                allt[sl.key] = (sl.sem, sl.cnt)
        for eng in (engines or list(self.E.keys())):
            self._wait(eng, allt)


class T:
    _n = [0]

    def __init__(self, es, nc, name, shape, dtype, psum=False):
        T._n[0] += 1
        name = "t%d_%s" % (T._n[0], name)
        self.t = es.enter_context((nc.psum_tensor if psum else nc.sbuf_tensor)(name, shape, dtype))
        self.b = Buf(name)

    def __getitem__(self, idx):
        return self.t[idx]


def rot(lst):
    i = 0
    while True:
        yield lst[i % len(lst)]
        i += 1


def build(n_layers=NL, debug=False, upto=None):
    nc = bass.Bass("TRN2", target_bir_lowering=False)
    dk = "ExternalOutput" if debug else "Internal"

    def din(name, shape, dt=F32):
        return nc.dram_tensor(name, shape, dt, kind="ExternalInput").ap()

    def dscr(name, shape, dt):
        return nc.dram_tensor(name, shape, dt, kind=dk).ap()

    xT = din("xT", [D, S])
    prm_d = din("prm", [128, 2 * PCL + 8])
    cst_d = din("cst", [128, 128 + 32 + 32 + 2048])
    ohrev_d = din("ohrev", [128, FR])
    relb_d = din("rel_bias", [32, 8])
    w_in = din("w_in", [NL, D, 7 * D])
    lru_wa = din("lru_wa", [NL, 8, 128, 128])
    lru_wx = din("lru_wx", [NL, 8, 128, 128])
    w_ro = din("w_rnn_out", [NL, D, D])
    w_ao = din("w_attn_out", [NL, D, D])
    w_o = din("w_o", [NL, D, D])
    w_up = din("w_up", [NL, D, 2 * DFF])
    w_dn = din("w_down", [NL, DFF, D])
    outT = nc.dram_tensor("outT", [D, S], F32, kind="ExternalOutput").ap()

    XS = dscr("XS", [D, S], F32)
    QT = dscr("QT", [D, S], BF16)
    KT = dscr("KT", [D, S], BF16)
    XR = dscr("XR", [D, S], BF16)
    GY = dscr("GY", [D, S], BF16)
    SGA = dscr("SGA", [D, S], BF16)
    SGB = dscr("SGB", [D, S], BF16)
    YAG = dscr("YAG", [D, S], BF16)
    ATT = dscr("ATT", [D, S], BF16)
    VV = dscr("VV", [S, D], BF16)
    ACTS = dscr("ACTS", [DFF, S], BF16)
    FREV = dscr("FREV", [8, FR], BF16)
    DBG = dscr("DBG", [128, 8192], BF16) if debug else None
    dbuf = {nm: Buf(nm) for nm in ["QT", "KT", "XR", "GY", "SGA", "SGB", "YAG", "ATT", "VV", "ACTS", "FREV"]}
    xs_b512 = None

    def fm(ap, c0, c1):
        return ap.rearrange("(c p) t -> p c t", p=128)[:, :, c0:c1]

    with contextlib.ExitStack() as es:
        sc = Sched(nc, es)
        G = lambda name, shape, dt, psum=False: T(es, nc, name, shape, dt, psum)
        class V:
            def __init__(self, ap, name):
                self.ap = ap
                self.b = Buf(name)

            def __getitem__(self, idx):
                return self.ap[idx]

        def halias(k0, nk, shape, dt, name):
            ap = H.t[:, k0:k0 + nk, :].rearrange("p a b -> p (a b)")
            if dt == F32:
                ap = ap.bitcast(F32)
            if len(shape) == 3:
                ap = ap.rearrange("p (a b) -> p a b", b=shape[2])
            assert list(ap.shape) == list(shape), (ap.shape, shape)
            return V(ap, name)

        H = G("H", [128, 8, S], BF16)
        Hb = [Buf("H%d" % i) for i in range(16)]
        WG = [G("WG%d" % i, [128, 8, 1024], BF16) for i in range(3)]
        prm = G("prm", [128, 2 * PCL + 8], F32)
        cst = G("cst", [128, 192], F32)
        identb = G("identb", [128, 128], BF16)
        onesb = G("onesb", [128, 128], BF16)
        oh16 = G("oh16", [128, 2048], BF16)
        tabbc = G("tabbc", [128, 256], F32)
        PS = [G("ps%d" % i, [128, 512], F32, psum=True) for i in range(7)]
        ps7 = G("ps7", [128, 512], F32, psum=True)
        psG = V(ps7[:, 0:32], "psG")
        psT = V(ps7[:, 32:288], "psT")
        psA = V(ps7[:, 288:416].bitcast(BF16), "psA")
        psT.b = psG.b
        psA.b = psG.b
        identf = cst[:, 0:128]
        NM = lambda i: cst[:, 128 + 16 - i: 128 + 32 - i]
        FM = lambda i: cst[:, 160 + 16 - (i - 4): 160 + 32 - (i - 4)]

        ld = sc.slot("ld_misc")
        sc.dma("sp", ld, prm[:, :], prm_d[:, :], writes=[prm.b])
        sc.dma("sp", ld, cst[:, :], cst_d[:, 0:192], writes=[cst.b])
        ohtmp = halias(0, 1, [128, 2048], F32, "ohtmp")
        sc.dma("sp", ld, ohtmp[:, :], cst_d[:, 192:2240], writes=[ohtmp.b])
        sc.dma("sp", ld, tabbc[:, :], bass.AP(tensor=relb_d.tensor, offset=0, ap=[[0, 128], [1, 256]]), writes=[tabbc.b])
        sc.barrier()
        sc.op("dve", lambda h: h.tensor_copy(out=identb[:, :], in_=cst[:, 0:128]), reads=[cst.b], writes=[identb.b])
        sc.op("dve", lambda h: h.tensor_copy(out=oh16[:, :], in_=ohtmp[:, :]), reads=[ohtmp.b], writes=[oh16.b])
        sc.op("dve", lambda h: h.memset(onesb[:, :], 1.0), writes=[onesb.b])
        sc.barrier()

        def P(l, name, c=None):
            offs = dict(n1g=0, cw=8, cb=40, ba=48, bx=56, lam=64, n2g=72, fw=80, fb=152)
            base = l * PCL + offs[name]
            if name == "fin":
                base = 2 * PCL
            return base

        def pcol(l, name, idx):
            if name == "fin":
                b0 = 2 * PCL + idx
            else:
                b0 = P(l, name) + idx
            return prm[:, b0:b0 + 1]

        wslot = [sc.slot("w%d" % i) for i in range(3)]

        def wload(i, dst_ap, src_ap):
            sc.dma("pool", wslot[i], dst_ap, src_ap, writes=[WG[i].b])

        def wgroup(i, w_l, r0, c0, ncols=1024, dcol=0):
            src = w_l[r0:r0 + 1024, :].rearrange("(mc p) n -> p mc n", p=128)[:, :, c0:c0 + ncols]
            sc.dma("pool", wslot[i], WG[i][:, :, dcol:dcol + ncols], src, acc=[WG[i].b])

        def norm_a(xt, xb, sq):
            sc.op("act", lambda h: h.activation(out=sq[:, :, :], in_=xt[:, :, :], func=AF.Square), reads=[xb], writes=[sq.b])

        def norm_tile(xt, xb, ncols, col0, gname, gl, sq, rs, rstd, psn, out_f32=None):
            norm_a(xt, xb, sq)
            norm_b(xt, xb, ncols, col0, gname, gl, sq, rs, rstd, psn, out_f32)

        def norm_b(xt, xb, ncols, col0, gname, gl, sq, rs, rstd, psn, out_f32=None):
            sc.op("pe", [(lambda h, c=c: h.matmul(psn[:, 0:ncols], lhsT=onesb[:, :], rhs=sq[:, c, :], start=(c == 0), stop=(c == 7)))
                         for c in range(8)], reads=[sq.b, onesb.b], writes=[psn.b])
            sc.op("act", lambda h: h.activation(out=rs[:, :], in_=psn[:, 0:ncols], func=AF.Sqrt, bias=EPS, scale=1.0 / D),
                  reads=[psn.b], writes=[rs.b])
            sc.op("dve", lambda h: h.reciprocal(out=rstd[:, :], in_=rs[:, :]), reads=[rs.b], writes=[rstd.b])
            hbs = [Hb[(col0 + k * 256) // 256] for k in range(ncols // 256)]
            for c in range(8):
                if out_f32 is None:
                    sc.op("dve", lambda h, c=c: h.scalar_tensor_tensor(out=H[:, c, col0:col0 + ncols], in0=xt[:, c, :],
                                                                         scalar=pcol(gl, gname, c), in1=rstd[:, :],
                                                                         op0=ALU.mult, op1=ALU.mult),
                          reads=[xb, rstd.b, prm.b], acc=hbs)
                else:
                    sc.op("dve", lambda h, c=c: h.scalar_tensor_tensor(out=out_f32[:, c, :], in0=xt[:, c, :],
                                                                         scalar=pcol(gl, gname, c), in1=rstd[:, :],
                                                                         op0=ALU.mult, op1=ALU.mult),
                          reads=[xb, rstd.b, prm.b], acc=[out_f32.b])

        def hb_of(col0, ncols):
            return [Hb[(col0 + k * 256) // 256] for k in range(max(1, ncols // 256))]

        def phase_bias():
            with contextlib.ExitStack() as es2:
                L = lambda name, shape, dt: T(es2, nc, name, shape, dt)
                tabx = L("tabx", [128, 128], F32)
                ohr = L("ohr", [128, FR], F32)
                frs = L("frs", [128, FR], BF16)
                sc.op("dve", lambda h: h.memset(tabx[:, :], 0.0), writes=[tabx.b])
                sc.op("dve", lambda h: h.memset(tabx[32:33, 0:8], NEGM), reads=[tabx.b], writes=[tabx.b])
                sc.dma("sp", ld, tabx[0:32, 0:8], relb_d[:, :], reads=[tabx.b], writes=[tabx.b])
                sc.dma("sp", ld, ohr[:, :], ohrev_d[:, :], writes=[ohr.b])
                for k in range(3):
                    sc.op("pe", lambda h, k=k: h.matmul(PS[k][:, :], lhsT=tabx[:, :], rhs=ohr[:, k * 512:(k + 1) * 512], start=True, stop=True),
                          reads=[tabx.b, ohr.b], writes=[PS[k].b])
                    sc.op("dve", lambda h, k=k: h.tensor_copy(out=frs[:, k * 512:(k + 1) * 512], in_=PS[k][:, :]), reads=[PS[k].b], acc=[frs.b])
                sc.dma("pool", sc.slot("st0"), FREV[:, :], frs[0:8, :], reads=[frs.b], acc=[dbuf["FREV"]])
                sc.barrier()

        def phase_n0(l, xsrc):
            with contextlib.ExitStack() as es2:
                L = lambda name, shape, dt: T(es2, nc, name, shape, dt)
                xt = [L("n0x%d" % i, [128, 8, 512], F32) for i in range(2)]
                sq = L("n0sq", [128, 8, 512], BF16)
                rs = L("n0rs", [128, 512], F32)
                rstd = L("n0rstd", [128, 512], F32)
                sl = [sc.slot("ldA0"), sc.slot("ldA1")]
                for t in range(8):
                    x_ = xt[t % 2]
                    sc.dma("sp", sl[t % 2], x_[:, :, :], fm(xsrc, t * 512, (t + 1) * 512), writes=[x_.b])
                    norm_tile(x_, x_.b, 512, t * 512, "n1g", l, sq, rs, rstd, PS[6])
                sc.barrier()

        GROUPS = [("k", 3072), ("v", 4096), ("q", 2048), ("xr", 0), ("yr", 1024), ("ga", 5120), ("gb", 6144)]

        def p_wload(l, gi):
            wgroup(gi % 3, w_in[l], 0, GROUPS[gi][1])

        def phase_p(l, first=False):
            with contextlib.ExitStack() as es2:
                L = lambda name, shape, dt: T(es2, nc, name, shape, dt)
                if first:
                    tabx = L("tabx", [128, 128], F32)
                    ohr = L("ohr", [128, FR], F32)
                    frs = L("frs", [128, FR], BF16)
                    sc.op("dve", lambda h: h.memset(tabx[:, :], 0.0), writes=[tabx.b])
                    sc.op("dve", lambda h: h.memset(tabx[32:33, 0:8], NEGM), reads=[tabx.b], writes=[tabx.b])
                    sc.dma("sp", sc.slot("ldB0"), tabx[0:32, 0:8], relb_d[:, :], reads=[tabx.b], writes=[tabx.b])
                    sc.dma("sp", sc.slot("ldB1"), ohr[:, :], ohrev_d[:, :], writes=[ohr.b])
                    for k in range(3):
                        sc.op("pe", lambda h, k=k: h.matmul(PS[k][:, :], lhsT=tabx[:, :], rhs=ohr[:, k * 512:(k + 1) * 512], start=True, stop=True),
                              reads=[tabx.b, ohr.b], writes=[PS[k].b])
                        sc.op("dve", lambda h, k=k: h.tensor_copy(out=frs[:, k * 512:(k + 1) * 512], in_=PS[k][:, :]), reads=[PS[k].b], acc=[frs.b])
                    sc.dma("pool", sc.slot("st2"), FREV[:, :], frs[0:8, :], reads=[frs.b], acc=[dbuf["FREV"]])
                    n0x = [L("n0x%d" % i, [128, 8, 512], F32) for i in range(2)]
                    n0sq = [L("n0sq%d" % i, [128, 8, 512], BF16) for i in range(2)]
                    n0rs = L("n0rs", [128, 512], F32)
                    n0rstd = L("n0rstd", [128, 512], F32)
                    n0sl = [sc.slot("ldA0"), sc.slot("ldA1")]

                    def n0_load(t):
                        sc.dma("sp", n0sl[t % 2], n0x[t % 2][:, :, :], fm(xT, t * 512, (t + 1) * 512), writes=[n0x[t % 2].b])
                        norm_a(n0x[t % 2], n0x[t % 2].b, n0sq[t % 2])

                    def n0_norm(t):
                        norm_b(n0x[t % 2], n0x[t % 2].b, 512, t * 512, "n1g", 0, n0sq[t % 2], n0rs, n0rstd, PS[6])
                    n0_load(0)
                    n0_load(1)
                    n0_norm(0)
                stg = [L("pstg%d" % i, [128, 8, 512], BF16) for i in range(2)]
                vst = [L("pvst%d" % i, [128, 1024], BF16) for i in range(2)]
                sts = [sc.slot("st0"), sc.slot("st1")]
                banks = rot(PS[0:6])
                dst = dict(k=KT, q=QT, xr=XR, yr=GY, ga=SGA, gb=SGB)
                dbn = dict(k="KT", q="QT", xr="XR", yr="GY", ga="SGA", gb="SGB")
                nst = 0
                for gi, (gname, c0) in enumerate(GROUPS):
                    if gi + 2 < len(GROUPS):
                        p_wload(l, gi + 2)
                    if gi == 5:
                        wgroup(1, w_ro[l], 0, 0)
                    if gi == 6:
                        wgroup(2, w_ao[l], 0, 0)
                    W = WG[gi % 3]
                    if gname == "v":
                        for tt in range(32):
                            vs = vst[tt % 2]
                            for half in range(2):
                                bk = next(banks)
                                sc.op("pe", [(lambda h, mc=mc: h.matmul(bk[:, :], lhsT=H[:, mc, tt * 128:(tt + 1) * 128],
                                                                          rhs=W[:, mc, half * 512:(half + 1) * 512],
                                                                          start=(mc == 0), stop=(mc == 7))) for mc in range(8)],
                                      reads=[W.b] + hb_of(tt * 128, 128), writes=[bk.b])
                                sc.op("dve", lambda h: h.tensor_copy(out=vs[:, half * 512:(half + 1) * 512], in_=bk[:, :]),
                                      reads=[bk.b], acc=[vs.b])
                            sc.dma("pool", sts[nst % 2], VV[tt * 128:(tt + 1) * 128, :], vs[:, :], reads=[vs.b], acc=[dbuf["VV"]])
                            nst += 1
                        continue
                    for t in range(8):
                        sg = stg[t % 2]
                        if first and gi == 0:
                            if t + 1 < 8:
                                n0_norm(t + 1)
                            if t + 2 < 8:
                                n0_load(t + 2)
                        for n in range(8):
                            bk = next(banks)
                            sc.op("pe", [(lambda h, mc=mc: h.matmul(bk[:, :], lhsT=W[:, mc, n * 128:(n + 1) * 128],
                                                                      rhs=H[:, mc, t * 512:(t + 1) * 512],
                                                                      start=(mc == 0), stop=(mc == 7))) for mc in range(8)],
                                  reads=[W.b] + hb_of(t * 512, 512), writes=[bk.b])
                            if gname == "q":
                                sc.op("dve", lambda h: h.tensor_scalar(out=sg[:, n, :], in0=bk[:, :], scalar1=128.0 ** -0.5, scalar2=None,
                                                                         op0=ALU.mult), reads=[bk.b], acc=[sg.b])
                            elif gname in ("k", "xr"):
                                sc.op("dve", lambda h: h.tensor_copy(out=sg[:, n, :], in_=bk[:, :]), reads=[bk.b], acc=[sg.b])
                            elif gname == "yr":
                                sc.op("act", lambda h: h.activation(out=sg[:, n, :], in_=bk[:, :], func=AF.Gelu_apprx_tanh),
                                      reads=[bk.b], acc=[sg.b])
                            else:
                                sc.op("act", lambda h: h.activation(out=sg[:, n, :], in_=bk[:, :], func=AF.Sigmoid),
                                      reads=[bk.b], acc=[sg.b])
                        sc.dma("pool", sts[nst % 2], fm(dst[gname], t * 512, (t + 1) * 512), sg[:, :, :], reads=[sg.b], acc=[dbuf[dbn[gname]]])
                        nst += 1
                sc.barrier()

        def phase_r(l):
            with contextlib.ExitStack() as es2:
                L = lambda name, shape, dt: T(es2, nc, name, shape, dt)
                wa = L("wa", [128, 8, 128], BF16)
                wx = L("wx", [128, 8, 128], BF16)
                dgw = L("dgw", [128, 32, 128], BF16)
                WRO = WG[1]
                sc.dma("pool", sc.slot("wsm0"), wa[:, :, :], lru_wa[l].rearrange("c d e -> d c e"), writes=[wa.b])
                sc.dma("pool", sc.slot("wsm1"), wx[:, :, :], lru_wx[l].rearrange("c d e -> d c e"), writes=[wx.b])
                cwb = P(l, "cw")
                for kc in range(32):
                    sc.op("dve", lambda h, kc=kc: h.tensor_scalar(out=dgw[:, kc, :], in0=identf, scalar1=prm[:, cwb + kc:cwb + kc + 1], scalar2=None, op0=ALU.mult),
                          reads=[cst.b, prm.b], acc=[dgw.b])
                ls = L("ls", [128, 24], F32)
                lb = P(l, "lam")
                sc.op("act", lambda h: h.activation(out=ls[:, 0:8], in_=prm[:, lb:lb + 8], func=AF.Exp, scale=-1.0), reads=[prm.b], writes=[ls.b])
                sc.op("act", lambda h: h.activation(out=ls[:, 8:16], in_=ls[:, 0:8], func=AF.Ln, bias=1.0), reads=[ls.b], writes=[ls.b])
                sc.op("dve", lambda h: h.tensor_scalar(out=ls[:, 16:24], in0=ls[:, 8:16], scalar1=-4.0, scalar2=None, op0=ALU.mult), reads=[ls.b], writes=[ls.b])
                sc.op("dve", lambda h: h.tensor_scalar(out=ls[:, 8:16], in0=ls[:, 8:16], scalar1=-8.0, scalar2=None, op0=ALU.mult), reads=[ls.b], writes=[ls.b])
                hb = L("hb", [128, 16], F32)
                bab = P(l, "ba")
                sc.op("dve", lambda h: h.tensor_scalar(out=hb[:, :], in0=prm[:, bab:bab + 16], scalar1=0.5, scalar2=None, op0=ALU.mult), reads=[prm.b], writes=[hb.b])
                hst = L("hst", [128, 8], F32)
                hstb = [Buf("hst%d" % c) for c in range(8)]
                sc.op("dve", lambda h: h.memset(hst[:, :], 0.0), writes=hstb)
                xrw = [L("xrw%d" % i, [128, 8, 515], BF16) for i in range(2)]
                gy = [L("gy%d" % i, [128, 8, 512], BF16) for i in range(2)]
                sga = [L("sga%d" % i, [128, 8, 512], BF16) for i in range(2)]
                rr = halias(0, 2, [128, 8, 512], F32, "rr")
                aa = halias(2, 2, [128, 8, 512], F32, "aa")
                ig = halias(4, 1, [128, 8, 512], BF16, "ig")
                xcb = halias(5, 1, [128, 8, 512], BF16, "xcb")
                z = halias(6, 1, [128, 8, 512], BF16, "z")
                yst = [halias(7, 1, [128, 8, 512], BF16, "yst0"), L("yst1", [128, 8, 512], BF16)]
                rrB = [Buf("rr%d" % c) for c in range(8)]
                aaB = [Buf("aa%d" % c) for c in range(8)]
                igB = [Buf("ig%d" % c) for c in range(8)]
                xcB = [Buf("xc%d" % c) for c in range(8)]
                zB = [Buf("z%d" % c) for c in range(8)]
                tq = L("tq", [128, 8, 512], BF16)
                tqB = [Buf("tq%d" % c) for c in range(8)]
                uu = [L("uu%d" % i, [128, 512], BF16) for i in range(2)]
                hh = [L("hh%d" % i, [128, 512], F32) for i in range(2)]
                sl = [[sc.slot("ldA%d" % i), sc.slot("ldB%d" % i), sc.slot("ldC%d" % i)] for i in range(2)]
                sts = [sc.slot("st0"), sc.slot("st1")]
                sc.op("dve", lambda h: h.memset(xrw[0][:, :, 0:3], 0.0), writes=[xrw[0].b])

                def loads(t):
                    i = t % 2
                    if t == 0:
                        sc.dma("sp", sl[i][0], xrw[i][:, :, 3:515], fm(XR, 0, 512), reads=[dbuf["XR"]], writes=[xrw[i].b])
                    else:
                        sc.dma("sp", sl[i][0], xrw[i][:, :, :], fm(XR, t * 512 - 3, (t + 1) * 512), reads=[dbuf["XR"]], writes=[xrw[i].b])
                    sc.dma("sp", sl[i][1], gy[i][:, :, :], fm(GY, t * 512, (t + 1) * 512), reads=[dbuf["GY"]], writes=[gy[i].b])
                    sc.dma("sp", sl[i][2], sga[i][:, :, :], fm(SGA, t * 512, (t + 1) * 512), reads=[dbuf["SGA"]], writes=[sga[i].b])
                cb_ = rot(PS[0:3])
                gb_ = rot(PS[3:5])
                yb_ = rot(PS[5:7])

                def conv(t, c):
                    X = xrw[t % 2]
                    bc = next(cb_)
                    sc.op("pe", [(lambda h, k=k: h.matmul(bc[:, :], lhsT=dgw[:, k * 8 + c, :], rhs=X[:, c, k:k + 512], start=(k == 0), stop=(k == 3)))
                                 for k in range(4)], reads=[dgw.b, X.b], writes=[bc.b])
                    sc.op("act", lambda h: h.activation(out=xcb[:, c, :], in_=bc[:, :], func=AF.Identity, bias=pcol(l, "cb", c)),
                          reads=[bc.b, prm.b], writes=[xcB[c]])

                def gates(t, c):
                    b1, b2 = next(gb_), next(gb_)
                    sc.op("pe", lambda h: h.matmul(b1[:, :], lhsT=wa[:, c, :], rhs=xcb[:, c, :], start=True, stop=True),
                          reads=[wa.b, xcB[c]], writes=[b1.b])
                    sc.op("pe", lambda h: h.matmul(b2[:, :], lhsT=wx[:, c, :], rhs=xcb[:, c, :], start=True, stop=True),
                          reads=[wx.b, xcB[c]], writes=[b2.b])
                    sc.op("act", lambda h: h.activation(out=rr[:, c, :], in_=b1[:, :], func=AF.Tanh, bias=hb[:, c:c + 1], scale=0.5),
                          reads=[b1.b, hb.b], writes=[rrB[c]])
                    sc.op("act", lambda h: h.activation(out=ig[:, c, :], in_=b2[:, :], func=AF.Tanh, bias=hb[:, 8 + c:9 + c], scale=0.5),
                          reads=[b2.b, hb.b], writes=[igB[c]])
                    sc.op("act", lambda h: h.activation(out=aa[:, c, :], in_=rr[:, c, :], func=AF.Exp, scale=ls[:, 16 + c:17 + c], bias=ls[:, 16 + c:17 + c]),
                          reads=[rrB[c], ls.b], writes=[aaB[c]])
                    sc.op("act", lambda h: h.activation(out=rr[:, c, :], in_=rr[:, c, :], func=AF.Exp, scale=ls[:, 8 + c:9 + c], bias=ls[:, 8 + c:9 + c]),
                          reads=[rrB[c], ls.b], writes=[rrB[c]])
                    sc.op("dve", lambda h: h.scalar_tensor_tensor(out=tq[:, c, :], in0=ig[:, c, :], scalar=1.0, in1=xcb[:, c, :], op0=ALU.add, op1=ALU.mult),
                          reads=[igB[c], xcB[c]], writes=[tqB[c]])

                def ya(t):
                    i = t % 2
                    ys = yst[i]
                    for n in range(8):
                        bk = next(yb_)
                        sc.op("pe", [(lambda h, c=c: h.matmul(bk[:, :], lhsT=WRO[:, c, n * 128:(n + 1) * 128], rhs=z[:, c, :],
                                                                start=(c == 0), stop=(c == 7))) for c in range(8)],
                              reads=[WRO.b] + zB, writes=[bk.b])
                        sc.op("dve", lambda h: h.tensor_tensor(out=ys[:, n, :], in0=bk[:, :], in1=sga[i][:, n, :], op=ALU.mult),
                              reads=[bk.b, sga[i].b], acc=[ys.b])
                    sc.dma("pool", sts[i], fm(YAG, t * 512, (t + 1) * 512), ys[:, :, :], reads=[ys.b], acc=[dbuf["YAG"]])

                loads(0)
                loads(1)
                for t in range(8):
                    i = t % 2
                    conv(t, 0)
                    conv(t, 1)
                    for c in range(8):
                        gates(t, c)
                        if c + 2 < 8:
                            conv(t, c + 2)
                    if t > 0:
                        ya(t - 1)
                        if t + 1 < 8:
                            loads(t + 1)
                    for c in range(8):
                        sc.op("act", lambda h: h.activation(out=rr[:, c, :], in_=rr[:, c, :], func=AF.Sqrt, bias=0.25, scale=-0.25),
                              reads=[rrB[c]], writes=[rrB[c]])
                    for c in range(8):
                        u_, h_ = uu[c % 2], hh[c % 2]
                        sc.op("dve", lambda h: h.tensor_tensor(out=u_[:, :], in0=rr[:, c, :], in1=tq[:, c, :], op=ALU.mult),
                              reads=[rrB[c], tqB[c]], writes=[u_.b])
                        sc.op("dve", lambda h: h.tensor_tensor_scan(out=h_[:, :], data0=aa[:, c, :], data1=u_[:, :], initial=hst[:, c:c + 1],
                                                                      op0=ALU.mult, op1=ALU.add),
                              reads=[aaB[c], u_.b, hstb[c]], writes=[h_.b])
                        sc.op("dve", lambda h: h.tensor_copy(out=hst[:, c:c + 1], in_=h_[:, 511:512]), reads=[h_.b], writes=[hstb[c]])
                        sc.op("dve", lambda h: h.tensor_tensor(out=z[:, c, :], in0=h_[:, :], in1=gy[i][:, c, :], op=ALU.mult),
                              reads=[h_.b, gy[i].b], writes=[zB[c]])
                ya(7)
                sc.barrier()

        def phase_a(l):
            with contextlib.ExitStack() as es2:
                L = lambda name, shape, dt: T(es2, nc, name, shape, dt)
                KTh = [halias(i, 1, [128, S], BF16, "kth%d" % i) for i in range(2)]
                QTh = [halias(2 + i, 1, [128, S], BF16, "qth%d" % i) for i in range(2)]
                Vh = [L("vh%d" % i, [128, 32, 129], BF16) for i in range(2)]
                Gt = [L("gt%d" % i, [128, GE], BF16) for i in range(2)]
                Gh = [L("gh%d" % i, [128, GE], BF16) for i in range(2)]
                ATh = [halias(4 + i, 1, [128, S], BF16, "ath%d" % i) for i in range(2)]
                mball = halias(6, 2, [128, 32, 256], BF16, "mball")
                km32 = L("km32", [128, 16], F32)
                kmb = [L("kmb%d" % i, [128, 16], BF16) for i in range(2)]
                mbT = [[V(mball[:, i * 16 + j, :], "mbT%d_%d" % (i, j)) for j in range(16)] for i in range(2)]
                gs = L("gs", [128, 2, 16], F32)
                top8 = L("top8", [128, 2, 8], F32)
                mb = L("mb", [128, 2, 16], F32)
                mb2 = L("mb2", [128, 2, 32], BF16)
                pt = [L("pt%d" % i, [128, 512], BF16) for i in range(4)]
                rc = L("rc", [128, 2], F32)
                an = [[L("an%d_%d" % (i, j), [128, 128], BF16) for j in range(2)] for i in range(2)]
                sl = [[sc.slot("ldA%d" % i), sc.slot("ldB%d" % i), sc.slot("ldC%d" % i), sc.slot("ldD%d" % i)] for i in range(2)]
                sts = [sc.slot("st0"), sc.slot("st1")]
                sbank = rot(PS[0:4])
                ptr = rot(pt)
                accb = [PS[4], PS[5]]
                psAT = V(PS[6][:, 0:128].bitcast(BF16), "psAT")
                psAT.b = PS[6].b
                sc.op("dve", lambda h: h.memset(mb2[:, :, :], 0.0), writes=[mb2.b])
                vinit = [Buf("vinit0"), Buf("vinit1")]
                for i in range(2):
                    sc.op("dve", lambda h, i=i: h.memset(Vh[i][:, :, 128:129], 1.0), writes=[Vh[i].b, vinit[i]])

                def loads(hd):
                    i = hd % 2
                    sc.dma("sp", sl[i][0], KTh[i][:, :], KT[hd * 128:(hd + 1) * 128, :], reads=[dbuf["KT"]], writes=[KTh[i].b])
                    sc.dma("sp", sl[i][1], QTh[i][:, :], QT[hd * 128:(hd + 1) * 128, :], reads=[dbuf["QT"]], writes=[QTh[i].b])
                    for q4 in range(4):
                        sc.dma("sp", sl[i][2], Vh[i][:, q4 * 8:(q4 + 1) * 8, 0:128],
                               VV.rearrange("(tt p) d -> p tt d", p=128)[:, q4 * 8:(q4 + 1) * 8, hd * 128:(hd + 1) * 128],
                               reads=[dbuf["VV"], vinit[i]], acc=[Vh[i].b])
                    fr = FREV[hd:hd + 1, 0:GE]
                    sc.dma("sp", sl[i][3], Gt[i][:, :], bass.AP(tensor=fr.tensor, offset=fr.offset, ap=[[1, 128], [1, GE]]),
                           reads=[dbuf["FREV"]], writes=[Gt[i].b])

                def loads_compute(hd):
                    i = hd % 2
                    ga = Gt[i][:, 0:GE]
                    rev = bass.AP(tensor=ga.tensor, offset=ga.offset + GE - 1, ap=[list(ga.ap[0]), [-1, GE]])
                    sc.op("dve", lambda h: h.tensor_copy(out=Gh[i][:, :], in_=rev), reads=[Gt[i].b], writes=[Gh[i].b])
                    sc.op("act", lambda h: h.activation(out=Gh[i][:, :], in_=Gh[i][:, :], func=AF.Exp), reads=[Gh[i].b], writes=[Gh[i].b])
                    sc.op("dve", lambda h: h.tensor_reduce(out=km32[:, :], in_=KTh[i][:, :].rearrange("p (n t) -> p n t", t=BLK),
                                                             axis=AX.X, op=ALU.add), reads=[KTh[i].b], writes=[km32.b])
                    sc.op("dve", lambda h: h.tensor_scalar(out=kmb[i][:, :], in0=km32[:, :], scalar1=1.0 / BLK, scalar2=None, op0=ALU.mult),
                          reads=[km32.b], writes=[kmb[i].b])

                def gate1(hd, qi):
                    i = hd % 2
                    for qh in range(2):
                        sc.op("pe", lambda h, qh=qh: h.matmul(psG[:, qh * 16:(qh + 1) * 16], lhsT=QTh[i][:, qi * 256 + qh * 128: qi * 256 + (qh + 1) * 128],
                                                               rhs=kmb[i][:, :], start=True, stop=True),
                              reads=[QTh[i].b, kmb[i].b], writes=[psG.b] if qh == 0 else (), acc=[psG.b] if qh == 1 else ())
                    nm_ = NM(qi)
                    nm2 = bass.AP(tensor=nm_.tensor, offset=nm_.offset, ap=[list(nm_.ap[0]), [0, 2], [1, 16]])
                    sc.op("dve", lambda h: h.tensor_tensor(out=gs[:, :, :], in0=psG[:, 0:32].rearrange("p (a b) -> p a b", a=2), in1=nm2, op=ALU.add),
                          reads=[psG.b, cst.b], writes=[gs.b])
                    for qh in range(2):
                        sc.op("dve", lambda h, qh=qh: h.max(out=top8[:, qh, :], in_=gs[:, qh, :]), reads=[gs.b],
                              writes=[top8.b] if qh == 0 else (), acc=[top8.b] if qh == 1 else ())
                    for qh in range(2):
                        sc.op("dve", lambda h, qh=qh: h.tensor_scalar(out=mb[:, qh, :], in0=gs[:, qh, :], scalar1=top8[:, qh, 2:3], scalar2=NEGM,
                                                                        op0=ALU.is_lt, op1=ALU.mult),
                              reads=[gs.b, top8.b], writes=[mb.b] if qh == 0 else (), acc=[mb.b] if qh == 1 else ())
                    fm_ = FM(qi)
                    fm2 = bass.AP(tensor=fm_.tensor, offset=fm_.offset, ap=[list(fm_.ap[0]), [0, 2], [1, 16]])
                    sc.op("dve", lambda h: h.scalar_tensor_tensor(out=mb2[:, :, 0:16], in0=fm2, scalar=tabbc[:, 31 * 8 + hd:31 * 8 + hd + 1],
                                                                    in1=mb[:, :, :], op0=ALU.mult, op1=ALU.add),
                          reads=[mb.b, cst.b, tabbc.b], writes=[mb2.b])

                def gate2(hd, qi):
                    i = hd % 2
                    m = mbT[i][qi]
                    for qh in range(2):
                        for b4 in range(4):
                            sc.op("dve", lambda h, qh=qh, b4=b4: h.transpose(
                                out=m[32 * b4:32 * b4 + 32, qh * 128 + 32 * b4: qh * 128 + 32 * b4 + 32],
                                in_=mb2[32 * b4:32 * b4 + 32, qh, :]),
                                reads=[mb2.b], writes=[m.b] if (qh == 0 and b4 == 0) else (), acc=[m.b] if not (qh == 0 and b4 == 0) else ())

                def s_step(hd, qi, j):
                    i = hd % 2
                    bk = next(sbank)
                    fns = []
                    rd = [KTh[i].b, QTh[i].b]
                    masked = (qi >= 4 and j < qi)
                    biased = (qi - j <= 4) and "bias" not in os.environ.get("A_SKIP", "")
                    for kh in range(2):
                        o = bk[:, kh * 256:(kh + 1) * 256]
                        nmm = 1 + int(masked)
                        cnt = [0]

                        def flags():
                            cnt[0] += 1
                            return dict(start=(cnt[0] == 1), stop=(cnt[0] == nmm))
                        ks = j * 256 + kh * 128
                        fl = flags()
                        fns.append(lambda h, o=o, ks=ks, fl=fl: h.matmul(o, lhsT=KTh[i][:, ks:ks + 128], rhs=QTh[i][:, qi * 256:(qi + 1) * 256], **fl))
                        if masked:
                            fl = flags()
                            fns.append(lambda h, o=o, fl=fl: h.matmul(o, lhsT=oh16[:, j * 128:(j + 1) * 128], rhs=mbT[i][qi][:, :], **fl))
                    if masked:
                        rd += [mbT[i][qi].b, oh16.b]
                    if "smm" not in os.environ.get("A_SKIP", ""):
                        sc.op("pe", fns, reads=rd, writes=[bk.b])
                    p_ = next(ptr)
                    if "exp" in os.environ.get("A_SKIP", ""):
                        return p_
                    sc.op("act", lambda h: h.activation(out=p_[:, :], in_=bk[:, :], func=AF.Exp), reads=[bk.b], writes=[p_.b])
                    if biased:
                        e0 = 256 * (qi - j) + 128
                        ga = Gh[i][:, e0:e0 + 256]
                        ev = bass.AP(tensor=ga.tensor, offset=ga.offset, ap=[list(ga.ap[0]), [-128, 2], [1, 256]])
                        pv3 = p_[:, :].rearrange("p (a b) -> p a b", a=2)
                        sc.op(os.environ.get("A_MULENG", "dve"), lambda h: h.tensor_tensor(out=pv3, in0=pv3, in1=ev, op=ALU.mult), reads=[p_.b, Gh[i].b], writes=[p_.b])
                    return p_

                def pv_step(hd, qi, j, p_):
                    i = hd % 2
                    if "pv" in os.environ.get("A_SKIP", ""):
                        return
                    a_ = accb[qi % 2]
                    fns = []
                    for qh in range(2):
                        khs = [0, 1]
                        if j == qi and qh == 0:
                            khs = [0]
                        for kh in khs:
                            first = (j == 0 and kh == 0 and qh == 0)
                            last = (j == qi and kh == khs[-1])
                            fns.append(lambda h, kh=kh, qh=qh, first=first, last=last: h.matmul(
                                a_[:, qh * 256:qh * 256 + 129], lhsT=p_[:, kh * 256 + qh * 128: kh * 256 + (qh + 1) * 128],
                                rhs=Vh[i][:, j * 2 + kh, :], start=first, stop=last, skip_group_check=True))
                    if j == 0:
                        sc.op("pe", fns, reads=[p_.b, Vh[i].b], writes=[a_.b])
                    else:
                        sc.op("pe", fns, reads=[p_.b, Vh[i].b], acc=[a_.b])

                def finish_a(hd, qi):
                    a_ = accb[qi % 2]
                    for qh in range(2):
                        n_ = an[qi % 2][qh]
                        sc.op("dve", lambda h, qh=qh, a_=a_: h.reciprocal(out=rc[:, qh:qh + 1], in_=a_[:, qh * 256 + 128:qh * 256 + 129]), reads=[a_.b], acc=[rc.b])
                        sc.op("dve", lambda h, qh=qh, a_=a_, n_=n_: h.tensor_scalar(out=n_[:, :], in0=a_[:, qh * 256:qh * 256 + 128], scalar1=rc[:, qh:qh + 1], scalar2=None, op0=ALU.mult),
                              reads=[a_.b, rc.b], writes=[n_.b])

                def finish_b(hd, qi):
                    i = hd % 2
                    for qh in range(2):
                        n_ = an[qi % 2][qh]
                        sc.op("pe", lambda h, qh=qh, n_=n_: h.transpose(psAT[:, qh * 128:(qh + 1) * 128], n_[:, :], identb[:, :]),
                              reads=[n_.b, identb.b], writes=[psAT.b] if qh == 0 else (), acc=[psAT.b] if qh == 1 else ())
                    sc.op("dve", lambda h, i=i, qi=qi: h.tensor_copy(out=ATh[i][:, qi * 256:(qi + 1) * 256], in_=psAT[:, :]), reads=[psAT.b], acc=[ATh[i].b])

                LOOK = int(os.environ.get("A_LOOK", "3"))
                NBG = int(os.environ.get("A_NBG", "3"))
                real_op = sc.op

                def capture(fn, *args):
                    cap = []
                    sc.op = lambda *a, **k: cap.append((a, k))
                    try:
                        fn(*args)
                    finally:
                        sc.op = real_op
                    return [(lambda a=a, k=k: real_op(*a, **k)) for (a, k) in cap]

                streams = {"G": [], "F": []}
                INF = (99, 99)

                def flush_upto(key):
                    for nm in ("G", "F"):
                        st_ = streams[nm]
                        last = -1
                        for idx, e in enumerate(st_):
                            if e[0] <= key:
                                last = idx
                        for _ in range(last + 1):
                            st_.pop(0)[1]()

                stepno = [0]

                def trickle():
                    stepno[0] += 1
                    n = 0
                    for _ in range(NBG):
                        for nm in ("G", "F"):
                            st_ = streams[nm]
                            if n < NBG and st_ and (len(st_[0]) < 3 or st_[0][2] <= stepno[0]):
                                st_.pop(0)[1]()
                                n += 1

                def enqueue_gate(hd_, q_):
                    streams["G"].extend([[(hd_, q_), e] for e in capture(gate1, hd_, q_)])
                    streams["G"].extend([[(hd_, q_), e] for e in capture(gate2, hd_, q_)])

                def store_head(hd_):
                    sc.dma("pool", sts[hd_ % 2], ATT[hd_ * 128:(hd_ + 1) * 128, :], ATh[hd_ % 2][:, :], reads=[ATh[hd_ % 2].b], acc=[dbuf["ATT"]])

                inflight = []

                def do_pv():
                    hd_, qi_, j_, p_ = inflight.pop(0)
                    if j_ == 0:
                        flush_upto((hd_, qi_))
                    pv_step(hd_, qi_, j_, p_)
                    trickle()
                    if j_ == qi_:
                        if qi_ >= 1:
                            streams["F"].extend([[INF, e] for e in capture(finish_b, hd_, qi_ - 1)])
                        nxt = (hd_, qi_ + 2) if qi_ + 2 <= 15 else (hd_ + 1, qi_ + 2 - 16)
                        streams["F"].extend([[nxt, e] for e in capture(finish_a, hd_, qi_)])
                        if qi_ == 15:
                            streams["F"].extend([[INF, e] for e in capture(finish_b, hd_, 15)])
                            streams["F"].append([INF, (lambda hd_=hd_: store_head(hd_))])
                            if hd_ + 2 < 8:
                                loads(hd_ + 2)
                                streams["G"].extend([[(hd_ + 2, 0), e, stepno[0] + 24] for e in capture(loads_compute, hd_ + 2)])
                        if hd_ + 1 < 8 and qi_ >= 4:
                            enqueue_gate(hd_ + 1, qi_)

                loads(0)
                loads_compute(0)
                loads(1)
                for pc in range(8):
                    streams["G"].append([(0, 4), (lambda pc=pc: real_op("dve", lambda h: h.memset(mball[:, pc * 4:(pc + 1) * 4, :], 0.0),
                                                                         writes=[mbT[pc // 4][(pc % 4) * 4 + k_].b for k_ in range(4)]))])
                streams["G"].extend([[(1, 0), e, 24] for e in capture(loads_compute, 1)])
                for q_ in range(4, 16):
                    enqueue_gate(0, q_)
                for hd in range(8):
                    for qi in range(16):
                        for j in range(qi + 1):
                            if j == 0 and (qi >= 4 or qi == 0):
                                flush_upto((hd, qi))
                            p_ = s_step(hd, qi, j)
                            inflight.append((hd, qi, j, p_))
                            if len(inflight) > LOOK:
                                do_pv()
                while inflight:
                    do_pv()
                flush_upto(INF)
                if DBG is not None:
                    sc.barrier()
                    sc.dma("sp", sc.slot("ld_misc"), DBG[:, :], mball[:, :, :].rearrange("p a b -> p (a b)"), reads=[mball.b], writes=[Buf()])
                sc.barrier()

        def phase_m(l, xsrc):
            with contextlib.ExitStack() as es2:
                L = lambda name, shape, dt: T(es2, nc, name, shape, dt)
                WAO, WO = WG[2], WG[0]
                att = [L("matt%d" % i, [128, 8, 256], BF16) for i in range(2)]
                yag = [L("myag%d" % i, [128, 8, 256], BF16) for i in range(2)]
                sgb = [L("msgb%d" % i, [128, 8, 256], BF16) for i in range(2)]
                xt = [L("mx%d" % i, [128, 8, 256], F32) for i in range(3)]
                mixed = [L("mixed%d" % i, [128, 8, 256], BF16) for i in range(2)]
                tmp = [L("mtmp%d" % i, [128, 256], F32) for i in range(2)]
                sq = [L("msq%d" % i, [128, 8, 256], BF16) for i in range(2)]
                rs = L("mrs", [128, 256], F32)
                rstd = L("mrstd", [128, 256], F32)
                sl = [[sc.slot("ldA%d" % i), sc.slot("ldB%d" % i), sc.slot("ldC%d" % i)] for i in range(2)]
                slx = [sc.slot("ldX%d" % i) for i in range(3)]
                sts = [sc.slot("st0"), sc.slot("st1"), sc.slot("st2")]
                banks = rot(PS[0:6])

                def loads(t):
                    i = t % 2
                    c0, c1 = t * 256, (t + 1) * 256
                    sc.dma("sp", sl[i][0], att[i][:, :, :], fm(ATT, c0, c1), reads=[dbuf["ATT"]], writes=[att[i].b])
                    sc.dma("sp", sl[i][1], yag[i][:, :, :], fm(YAG, c0, c1), reads=[dbuf["YAG"]], writes=[yag[i].b])
                    sc.dma("sp", sl[i][2], sgb[i][:, :, :], fm(SGB, c0, c1), reads=[dbuf["SGB"]], writes=[sgb[i].b])
                    rd = [] if xsrc is xT else [xs_b[t]]
                    sc.dma("sp", slx[t % 3], xt[t % 3][:, :, :], fm(xsrc, c0, c1), reads=rd, writes=[xt[t % 3].b])

                def stage_a(t):
                    i = t % 2
                    mx = mixed[i]
                    for n in range(8):
                        bk = next(banks)
                        sc.op("pe", [(lambda h, m=m: h.matmul(bk[:, 0:256], lhsT=WAO[:, m, n * 128:(n + 1) * 128], rhs=att[i][:, m, :],
                                                                start=(m == 0), stop=(m == 7))) for m in range(8)],
                              reads=[WAO.b, att[i].b], writes=[bk.b])
                        tm = tmp[n % 2]
                        sc.op("dve", lambda h: h.tensor_tensor(out=tm[:, :], in0=bk[:, 0:256], in1=sgb[i][:, n, :], op=ALU.mult),
                              reads=[bk.b, sgb[i].b], writes=[tm.b])
                        sc.op("dve", lambda h: h.tensor_tensor(out=mx[:, n, :], in0=tm[:, :], in1=yag[i][:, n, :], op=ALU.add),
                              reads=[tm.b, yag[i].b], acc=[mx.b])

                def stage_b(t):
                    mx = mixed[t % 2]
                    x_ = xt[t % 3]
                    for n in range(8):
                        bk = next(banks)
                        sc.op("pe", [(lambda h, m=m: h.matmul(bk[:, 0:256], lhsT=WO[:, m, n * 128:(n + 1) * 128], rhs=mx[:, m, :],
                                                                start=(m == 0), stop=(m == 7))) for m in range(8)],
                              reads=[WO.b, mx.b], writes=[bk.b])
                        sc.op("dve", lambda h: h.tensor_tensor(out=x_[:, n, :], in0=bk[:, 0:256], in1=x_[:, n, :], op=ALU.add),
                              reads=[bk.b, x_.b], acc=[x_.b])
                    sc.dma("pool", sts[t % 3], fm(XS, t * 256, (t + 1) * 256), x_[:, :, :], reads=[x_.b], writes=[xs_b[t]])
                    norm_a(x_, x_.b, sq[t % 2])

                def stage_n(t):
                    x_ = xt[t % 3]
                    norm_b(x_, x_.b, 256, t * 256, "n2g", l, sq[t % 2], rs, rstd, PS[6])
                loads(0)
                loads(1)
                stage_a(0)
                for t in range(16):
                    if t + 1 < 16:
                        stage_a(t + 1)
                    stage_b(t)
                    if t >= 1:
                        stage_n(t - 1)
                    if t + 2 < 16:
                        loads(t + 2)
                stage_n(15)
                sc.barrier()

        def f1_wload(l, cg):
            i = (cg + 1) % 3
            src_g = w_up[l].rearrange("(mc p) n -> p mc n", p=128)[:, :, cg * 512:(cg + 1) * 512]
            src_v = w_up[l].rearrange("(mc p) n -> p mc n", p=128)[:, :, DFF + cg * 512:DFF + (cg + 1) * 512]
            sc.dma("pool", wslot[i], WG[i][:, :, 0:512], src_g, acc=[WG[i].b])
            sc.dma("pool", wslot[i], WG[i][:, :, 512:1024], src_v, acc=[WG[i].b])

        def phase_f1(l):
            with contextlib.ExitStack() as es2:
                L = lambda name, shape, dt: T(es2, nc, name, shape, dt)
                gbuf = [[L("gb%d_%d" % (k, i), [128, 514], F32) for i in range(2)] for k in range(4)]
                c0_ = [L("fc0%d" % i, [128, 512], F32) for i in range(3)]
                c1_ = [L("fc1%d" % i, [128, 512], F32) for i in range(3)]
                gg = [L("fgg%d" % i, [128, 512], F32) for i in range(3)]
                ast = [L("fast%d" % i, [128, 4, 512], BF16) for i in range(3)]
                sts = [sc.slot("st0"), sc.slot("st1"), sc.slot("st2")]
                banks = rot(PS[0:7])
                fwb = P(l, "fw")
                nst = 0
                nq = 0
                pend = None

                def flush(p):
                    a_, k, bv, g_, store = p
                    sc.op("dve", lambda h: h.tensor_tensor(out=a_[:, k, :], in0=bv[:, :], in1=g_[:, :], op=ALU.mult),
                          reads=[bv.b, g_.b], acc=[a_.b])
                    if store is not None:
                        dstap, slot = store
                        sc.dma("pool", slot, dstap, a_[:, :, :], reads=[a_.b], acc=[dbuf["ACTS"]])
                for cg in range(6):
                    if cg + 2 < 6:
                        f1_wload(l, cg + 2)
                    if cg == 4:
                        wgroup(1, w_dn[l], 0, 0)
                    if cg == 5:
                        wgroup(2, w_dn[l], 1024, 0)
                    W = WG[(cg + 1) % 3]
                    for t in range(8):
                        a_ = ast[nst % 3]
                        for k in range(4):
                            c = cg * 4 + k
                            bg, bv = next(banks), next(banks)
                            sc.op("pe", [(lambda h, mc=mc: h.matmul(bg[:, :], lhsT=W[:, mc, k * 128:(k + 1) * 128], rhs=H[:, mc, t * 512:(t + 1) * 512],
                                                                      start=(mc == 0), stop=(mc == 7))) for mc in range(8)],
                                  reads=[W.b] + hb_of(t * 512, 512), writes=[bg.b])
                            sc.op("pe", [(lambda h, mc=mc: h.matmul(bv[:, :], lhsT=W[:, mc, 512 + k * 128:512 + (k + 1) * 128], rhs=H[:, mc, t * 512:(t + 1) * 512],
                                                                      start=(mc == 0), stop=(mc == 7))) for mc in range(8)],
                                  reads=[W.b] + hb_of(t * 512, 512), writes=[bv.b])
                            gbf = gbuf[k][t % 2]
                            gpv = gbuf[k][(t + 1) % 2]
                            if t == 0:
                                sc.op("dve", lambda h: h.memset(gbf[:, 0:2], 0.0), writes=[gbf.b])
                            else:
                                sc.op("dve", lambda h: h.tensor_copy(out=gbf[:, 0:2], in_=gpv[:, 512:514]), reads=[gpv.b], writes=[gbf.b])
                            sc.op("act", lambda h: h.activation(out=gbf[:, 2:514], in_=bg[:, :], func=AF.Copy), reads=[bg.b], acc=[gbf.b])
                            x0, x1, g_ = c0_[nq % 3], c1_[nq % 3], gg[nq % 3]
                            nq += 1
                            sc.op("act", lambda h: h.activation(out=x0[:, :], in_=gbf[:, 0:512], func=AF.Identity,
                                                                  scale=prm[:, fwb + c:fwb + c + 1], bias=pcol(l, "fb", c)),
                                  reads=[gbf.b, prm.b], writes=[x0.b])
                            sc.op("dve", lambda h: h.scalar_tensor_tensor(out=x1[:, :], in0=gbf[:, 1:513], scalar=prm[:, fwb + 24 + c:fwb + 25 + c],
                                                                            in1=x0[:, :], op0=ALU.mult, op1=ALU.add),
                                  reads=[gbf.b, prm.b, x0.b], writes=[x1.b])
                            sc.op("dve", lambda h: h.scalar_tensor_tensor(out=x0[:, :], in0=gbf[:, 2:514], scalar=prm[:, fwb + 48 + c:fwb + 49 + c],
                                                                            in1=x1[:, :], op0=ALU.mult, op1=ALU.add),
                                  reads=[gbf.b, prm.b, x1.b], writes=[x0.b])
                            sc.op("act", lambda h: h.activation(out=g_[:, :], in_=x0[:, :], func=AF.Gelu_apprx_tanh), reads=[x0.b], writes=[g_.b])
                            if pend is not None:
                                flush(pend)
                            store = None
                            if k == 3:
                                dstap = ACTS[cg * 512:(cg + 1) * 512, :].rearrange("(c p) t -> p c t", p=128)[:, :, t * 512:(t + 1) * 512]
                                store = (dstap, sts[nst % 3])
                            pend = (a_, k, bv, g_, store)
                        nst += 1
                flush(pend)
                sc.barrier()

        def phase_f2(l, last):
            with contextlib.ExitStack() as es2:
                L = lambda name, shape, dt: T(es2, nc, name, shape, dt)
                ac = [L("f2a%d" % i, [128, 24, 256], BF16) for i in range(2)]
                xt = [L("f2x%d" % i, [128, 8, 256], F32) for i in range(3)]
                ost = [L("f2o%d" % i, [128, 8, 256], F32) for i in range(2)] if last else None
                sq = [L("f2sq%d" % i, [128, 8, 256], BF16) for i in range(2)]
                rs = L("f2rs", [128, 256], F32)
                rstd = L("f2rstd", [128, 256], F32)
                sla = [sc.slot("ldA%d" % i) for i in range(2)]
                slx = [sc.slot("ldX%d" % i) for i in range(3)]
                sts = [sc.slot("st0"), sc.slot("st1"), sc.slot("st2")]
                banks = rot(PS[0:6])

                def loads(t):
                    c0, c1 = t * 256, (t + 1) * 256
                    sc.dma("sp", sla[t % 2], ac[t % 2][:, :, :], fm(ACTS, c0, c1), reads=[dbuf["ACTS"]], writes=[ac[t % 2].b])
                    sc.dma("sp", slx[t % 3], xt[t % 3][:, :, :], fm(XS, c0, c1), reads=[xs_b[t]], writes=[xt[t % 3].b])

                def stage_n(t):
                    x_ = xt[t % 3]
                    if not last:
                        norm_b(x_, x_.b, 256, t * 256, "n1g", l + 1, sq[t % 2], rs, rstd, PS[6])
                    else:
                        o_ = ost[t % 2]
                        norm_b(x_, x_.b, 256, t * 256, "fin", 0, sq[t % 2], rs, rstd, PS[6], out_f32=o_)
                        sc.dma("pool", sts[t % 3], fm(outT, t * 256, (t + 1) * 256), o_[:, :, :], reads=[o_.b], writes=[Buf()])
                loads(0)
                loads(1)
                for t in range(16):
                    i = t % 2
                    x_ = xt[t % 3]
                    for n in range(8):
                        bk = next(banks)
                        sc.op("pe", [(lambda h, m=m: h.matmul(bk[:, 0:256], lhsT=WG[(m // 8 + 1) % 3][:, m % 8, n * 128:(n + 1) * 128], rhs=ac[i][:, m, :],
                                                                start=(m == 0), stop=(m == 23))) for m in range(24)],
                              reads=[WG[0].b, WG[1].b, WG[2].b, ac[i].b], writes=[bk.b])
                        sc.op("dve", lambda h: h.tensor_tensor(out=x_[:, n, :], in0=bk[:, 0:256], in1=x_[:, n, :], op=ALU.add),
                              reads=[bk.b, x_.b], acc=[x_.b])
                    if not last:
                        sc.dma("pool", sts[t % 3], fm(XS, t * 256, (t + 1) * 256), x_[:, :, :], reads=[x_.b], writes=[xs_b[t]])
                    norm_a(x_, x_.b, sq[t % 2])
                    if t >= 1:
                        stage_n(t - 1)
                    if t + 2 < 16:
                        loads(t + 2)
                stage_n(15)
                sc.barrier()

        xs_b = [Buf("xs%d" % i) for i in range(16)]
        order = []
        for l in range(n_layers):
            order += [("p", l), ("r", l), ("a", l), ("m", l), ("f1", l), ("f2", l)]
        p_wload(0, 0)
        p_wload(0, 1)
        stop = False
        for (ph, l) in order:
            last = (l == n_layers - 1)
            if ph == "p":
                phase_p(l, first=(l == 0))
            elif ph == "r":
                wgroup(0, w_o[l], 0, 0)
                phase_r(l)
            elif ph == "a":
                phase_a(l)
                f1_wload(l, 0)
            elif ph == "m":
                phase_m(l, xT if l == 0 else XS)
                f1_wload(l, 1)
            elif ph == "f1":
                phase_f1(l)
                wgroup(0, w_dn[l], 2048, 0)
            elif ph == "f2":
                phase_f2(l, last)
                if not last:
                    p_wload(l + 1, 0)
                    p_wload(l + 1, 1)
            if upto is not None and (ph, l) == upto:
                stop = True
                break
        sc.barrier()
        print("instructions:", sc.nins, "waits:", sc.nwait, "sems:", 5 + len(sc.slots),
              "counts:", {k: v["cnt"] for k, v in sc.E.items()})
    return nc


def t5_bucket_np(n):
    n = np.maximum(n, 0)
    nf = np.maximum(n, 1).astype(np.float32)
    v = (np.log(nf / np.float32(16.0)) / np.float32(math.log(64.0)) * np.float32(16.0))
    large = 16 + v.astype(np.int32)
    large = np.minimum(large, 31)
    return np.where(n < 16, n, large)


def host_consts():
    cst = np.zeros((128, 2240), np.float32)
    cst[:, 0:128] = np.eye(128, dtype=np.float32)
    cst[:, 128 + 16:128 + 32] = -1e30
    cst[:, 160:160 + 16] = 1.0
    for j in range(16):
        for b in range(4):
            cst[32 * b + j, 192 + j * 128:192 + (j + 1) * 128] = 1.0
    oh = np.zeros((128, FR), np.float32)
    u = np.arange(FR)
    d = 1279 - u
    bk = t5_bucket_np(d)
    for uu in range(1535):
        if d[uu] >= 0:
            oh[bk[uu], uu] = 1.0
        else:
            oh[32, uu] = 1.0
    return cst, oh


def host_params(inp):
    prm = np.zeros((128, 2 * PCL + 8), np.float32)

    def fmaj(v):
        return np.ascontiguousarray(v.reshape(-1, 128).T)
    for l in range(NL):
        b = l * PCL
        prm[:, b + 0:b + 8] = fmaj(inp["norm1_g"][l])
        for k in range(4):
            prm[:, b + 8 + k * 8:b + 16 + k * 8] = fmaj(inp["rnn_conv_w"][l][k])
        prm[:, b + 40:b + 48] = fmaj(inp["rnn_conv_b"][l])
        prm[:, b + 48:b + 56] = fmaj(inp["lru_ba"][l].reshape(-1))
        prm[:, b + 56:b + 64] = fmaj(inp["lru_bx"][l].reshape(-1))
        prm[:, b + 64:b + 72] = fmaj(inp["lru_lambda"][l])
        prm[:, b + 72:b + 80] = fmaj(inp["norm2_g"][l])
        for k in range(3):
            prm[:, b + 80 + k * 24:b + 104 + k * 24] = fmaj(inp["ffn_conv_w"][l][k])
        prm[:, b + 152:b + 176] = fmaj(inp["ffn_conv_b"][l])
    prm[:, 2 * PCL:2 * PCL + 8] = fmaj(inp["final_g"])
    return prm


def make_in_maps(inp, ncores=8):
    cst, oh = host_consts()
    prm = host_params(inp)
    shared = dict(prm=prm, cst=cst, ohrev=oh, rel_bias=np.ascontiguousarray(inp["rel_bias"], dtype=np.float32))
    for k in ["w_in", "lru_wa", "lru_wx", "w_rnn_out", "w_attn_out", "w_o", "w_up", "w_down"]:
        shared[k] = np.ascontiguousarray(inp[k], dtype=np.float32)
    maps = []
    for b in range(ncores):
        m = dict(shared)
        m["xT"] = np.ascontiguousarray(np.asarray(inp["x"][b], dtype=np.float32).T)
        maps.append(m)
    return maps


_NC_CACHE = {}


def kernel(**inputs):
    inp = {k: np.asarray(v) for k, v in inputs.items()}
    if "nc" not in _NC_CACHE:
        _NC_CACHE["nc"] = build()
    nc = _NC_CACHE["nc"]
    maps = make_in_maps(inp, 8)
    res = run_bass_kernel_spmd(nc, maps, core_ids=list(range(8)))
    out = np.stack([np.ascontiguousarray(r["outT"].T) for r in res.results], axis=0)
    return out.astype(np.float32)
```
